# Optimizing a Trainium2 kernel written in Bass

```python
import jax, jax.numpy as jnp
from jax import lax
import numpy as np

D_MODEL = 2048
BATCH = 2
SEQ = 8192
DEPTH = 4

HEAD_DIM = 128
D_MIX = D_MODEL
N_DIFF_HEADS = D_MIX // (2 * HEAD_DIM)
N_SB_HEADS = D_MIX // (2 * HEAD_DIM)
D_DIFF = N_DIFF_HEADS * HEAD_DIM
D_SB = N_SB_HEADS * HEAD_DIM
DIFF_QK_DIM = HEAD_DIM // 2
ROPE_DIM = DIFF_QK_DIM // 4
ROPE_THETA = 500000.0
D_IN = 3 * D_DIFF + 3 * D_SB
D_FF = (11 * D_MODEL) // 4
CONV_WIDTH = 3
Q_BLOCK = 128
NORM_EPS = 1e-6
SUBLN_EPS = 1e-5

kernel_name = "hymba_diff_stickbreaking_convffn_trunk"


def _rmsnorm(x, g, eps=NORM_EPS):
    xf = x.astype(jnp.float32)
    xf = xf * lax.rsqrt(jnp.mean(xf * xf, axis=-1, keepdims=True) + eps)
    return xf.astype(x.dtype) * g


def _rope_tables(seq_len):
    inv_freq = ROPE_THETA ** (-jnp.arange(0, ROPE_DIM, 2, dtype=jnp.float32) / ROPE_DIM)
    pos = jnp.arange(seq_len, dtype=jnp.float32)
    ang = pos[:, None] * inv_freq[None, :]
    return jnp.cos(ang), jnp.sin(ang)


def _partial_rope(x, cos, sin):
    half = ROPE_DIM // 2
    c = cos.astype(x.dtype)
    s = sin.astype(x.dtype)
    x1 = x[..., :half]
    x2 = x[..., half:ROPE_DIM]
    return jnp.concatenate([x1 * c - x2 * s, x2 * c + x1 * s, x[..., ROPE_DIM:]], axis=-1)


def _attention_blocks(q1, q2, k1, k2, vd, lam, qs, ks, vs):
    seq_len = q1.shape[2]
    diff_scale = DIFF_QK_DIM ** -0.5
    sb_scale = HEAD_DIM ** -0.5
    q_local = jnp.arange(Q_BLOCK)
    outs_d, outs_s = [], []
    for i in range(seq_len // Q_BLOCK):
        q0 = i * Q_BLOCK
        kend = q0 + Q_BLOCK
        qpos = q0 + q_local
        kpos = jnp.arange(kend)
        mask_incl = kpos[None, :] <= qpos[:, None]
        mask_strict = kpos[None, :] < qpos[:, None]

        s1 = jnp.einsum('bhqd,bhkd->bhqk', q1[:, :, q0:kend], k1[:, :, :kend]).astype(jnp.float32) * diff_scale
        s2 = jnp.einsum('bhqd,bhkd->bhqk', q2[:, :, q0:kend], k2[:, :, :kend]).astype(jnp.float32) * diff_scale
        p1 = jax.nn.softmax(jnp.where(mask_incl, s1, -jnp.inf), axis=-1)
        p2 = jax.nn.softmax(jnp.where(mask_incl, s2, -jnp.inf), axis=-1)
        w_diff = (p1 - lam * p2).astype(vd.dtype)
        outs_d.append(jnp.einsum('bhqk,bhkd->bhqd', w_diff, vd[:, :, :kend]))

        z = jnp.einsum('bhqd,bhkd->bhqk', qs[:, :, q0:kend], ks[:, :, :kend]).astype(jnp.float32) * sb_scale
        log_beta = jax.nn.log_sigmoid(z)
        log_1m = jnp.where(mask_strict, jax.nn.log_sigmoid(-z), 0.0)
        suffix = lax.cumsum(log_1m, axis=3, reverse=True) - log_1m
        a_sb = jnp.where(mask_strict, jnp.exp(log_beta + suffix), 0.0).astype(vs.dtype)
        outs_s.append(jnp.einsum('bhqk,bhkd->bhqd', a_sb, vs[:, :, :kend]))
    return jnp.concatenate(outs_d, axis=2), jnp.concatenate(outs_s, axis=2)


def _causal_dwconv(u, w, b):
    seq_len = u.shape[1]
    up = jnp.pad(u, ((0, 0), (CONV_WIDTH - 1, 0), (0, 0)))
    out = b
    for k in range(CONV_WIDTH):
        out = out + w[k] * up[:, k:k + seq_len]
    return out


def setup_inputs(seed: int = 0) -> dict:
    key = jax.random.key(seed)
    ks = jax.random.split(key, 17)
    f32 = jnp.float32
    nrm = lambda k, shape, scale: jax.random.normal(k, shape, f32) * scale
    gain = lambda k, shape: 1.0 + 0.05 * jax.random.normal(k, shape, f32)
    return {
        "x": jax.random.normal(ks[0], (BATCH, SEQ, D_MODEL), f32),
        "attn_pre_norm": gain(ks[1], (DEPTH, D_MODEL)),
        "w_in": nrm(ks[2], (DEPTH, D_MODEL, D_IN), D_MODEL ** -0.5),
        "diff_lambda_q1": nrm(ks[3], (DEPTH, DIFF_QK_DIM), 0.1),
        "diff_lambda_k1": nrm(ks[4], (DEPTH, DIFF_QK_DIM), 0.1),
        "diff_lambda_q2": nrm(ks[5], (DEPTH, DIFF_QK_DIM), 0.1),
        "diff_lambda_k2": nrm(ks[6], (DEPTH, DIFF_QK_DIM), 0.1),
        "diff_subln": gain(ks[7], (DEPTH, HEAD_DIM)),
        "w_out": nrm(ks[8], (DEPTH, D_MIX, D_MODEL), D_MIX ** -0.5),
        "attn_post_norm": gain(ks[9], (DEPTH, D_MODEL)),
        "ffn_pre_norm": gain(ks[10], (DEPTH, D_MODEL)),
        "ffn_w_up": nrm(ks[11], (DEPTH, D_MODEL, 2 * D_FF), D_MODEL ** -0.5),
        "ffn_conv_w": nrm(ks[12], (DEPTH, CONV_WIDTH, 2 * D_FF), CONV_WIDTH ** -0.5),
        "ffn_conv_b": nrm(ks[13], (DEPTH, 2 * D_FF), 0.01),
        "ffn_w_down": nrm(ks[14], (DEPTH, D_FF, D_MODEL), D_FF ** -0.5),
        "ffn_post_norm": gain(ks[15], (DEPTH, D_MODEL)),
    }


def reference(x, attn_pre_norm, w_in, diff_lambda_q1, diff_lambda_k1, diff_lambda_q2,
              diff_lambda_k2, diff_subln, w_out, attn_post_norm, ffn_pre_norm, ffn_w_up,
              ffn_conv_w, ffn_conv_b, ffn_w_down, ffn_post_norm):
    bsz, seq_len, _ = x.shape
    cos, sin = _rope_tables(seq_len)
    splits = [D_DIFF, 2 * D_DIFF, 3 * D_DIFF, 3 * D_DIFF + D_SB, 3 * D_DIFF + 2 * D_SB]
    for l in range(DEPTH):
        h = _rmsnorm(x, attn_pre_norm[l])
        proj = h @ w_in[l]
        dq, dk, dv, sq, sk, sv = jnp.split(proj, splits, axis=-1)

        dq = dq.reshape(bsz, seq_len, N_DIFF_HEADS, 2, DIFF_QK_DIM).transpose(3, 0, 2, 1, 4)
        dk = dk.reshape(bsz, seq_len, N_DIFF_HEADS, 2, DIFF_QK_DIM).transpose(3, 0, 2, 1, 4)
        dq = _partial_rope(dq, cos, sin)
        dk = _partial_rope(dk, cos, sin)
        dv = dv.reshape(bsz, seq_len, N_DIFF_HEADS, HEAD_DIM).transpose(0, 2, 1, 3)
        lambda_init = 0.8 - 0.6 * float(np.exp(-0.3 * l))
        lam = (jnp.exp(jnp.sum(diff_lambda_q1[l].astype(jnp.float32) * diff_lambda_k1[l].astype(jnp.float32)))
               - jnp.exp(jnp.sum(diff_lambda_q2[l].astype(jnp.float32) * diff_lambda_k2[l].astype(jnp.float32)))
               + lambda_init)

        sq = sq.reshape(bsz, seq_len, N_SB_HEADS, HEAD_DIM).transpose(0, 2, 1, 3)
        sk = sk.reshape(bsz, seq_len, N_SB_HEADS, HEAD_DIM).transpose(0, 2, 1, 3)
        sv = sv.reshape(bsz, seq_len, N_SB_HEADS, HEAD_DIM).transpose(0, 2, 1, 3)

        o_d, o_s = _attention_blocks(dq[0], dq[1], dk[0], dk[1], dv, lam, sq, sk, sv)
        o_d = _rmsnorm(o_d, diff_subln[l], SUBLN_EPS) * (1.0 - lambda_init)
        o_d = o_d.transpose(0, 2, 1, 3).reshape(bsz, seq_len, D_DIFF)
        o_s = o_s.transpose(0, 2, 1, 3).reshape(bsz, seq_len, D_SB)
        mix = jnp.concatenate([o_d, o_s], axis=-1) @ w_out[l]
        x = x + _rmsnorm(mix, attn_post_norm[l])

        h = _rmsnorm(x, ffn_pre_norm[l])
        u = _causal_dwconv(h @ ffn_w_up[l], ffn_conv_w[l], ffn_conv_b[l])
        gate, val = jnp.split(u, 2, axis=-1)
        y = (jax.nn.gelu(gate, approximate=True) * val) @ ffn_w_down[l]
        x = x + _rmsnorm(y, ffn_post_norm[l])
    return x
```

```python
import numpy as np
import ml_dtypes
import concourse.bass as bass
import concourse.mybir as mybir
from concourse.bass_utils import run_bass_kernel_spmd

F32 = mybir.dt.float32
BF16 = mybir.dt.bfloat16
AF = mybir.ActivationFunctionType
ALU = mybir.AluOpType

D_MODEL = 2048
BATCH = 2
SEQ = 8192
DEPTH = 4
D_IN = 6144
D_FF = 5632
NCORES = 8
TOK = 2048
HALO = 2
NORM_EPS = 1e-6
SUBLN_EPS = 1e-5
MASK_NEG = -30000.0
SEM_LIMIT = 30000


class Buf:
    __slots__ = ("w", "r", "name")

    def __init__(self, name=""):
        self.w = None
        self.r = []
        self.name = name


class Eng:
    def __init__(self, trk, name, eng):
        self.trk = trk
        self.name = name
        self.eng = eng
        self.sem = None
        self.count = 0
        self.seen = {}
        self.nsem = 0

    def cur_sem(self):
        if self.sem is None or self.count >= SEM_LIMIT:
            self.sem = self.trk.new_sem(f"{self.name}_{self.nsem}")
            self.nsem += 1
            self.count = 0
        return self.sem


class Tracker:
    def __init__(self, nc, stack, n_dma_sems=24):
        self.nc = nc
        self.stack = stack
        self.pe = Eng(self, "pe", nc.tensor)
        self.act = Eng(self, "act", nc.scalar)
        self.dve = Eng(self, "dve", nc.vector)
        self.pool = Eng(self, "pool", nc.gpsimd)
        self.sp = Eng(self, "sp", nc.sync)
        self.dma_sems = [self.new_sem(f"dma{i}") for i in range(n_dma_sems)]
        self.dma_cnt = [0] * n_dma_sems
        self.dma_rr = 0
        self.out_tickets = []

    def new_sem(self, name):
        return self.stack.enter_context(self.nc.semaphore(name))

    def _wait(self, E, ticket):
        sem, val, owner = ticket
        key = id(sem)
        if owner is E and E is self.pe:
            return
        if E.seen.get(key, 0) >= val:
            return
        E.eng.wait_ge(sem, val)
        E.seen[key] = val

    def _deps(self, E, reads, writes):
        for b in reads:
            if b.w is not None:
                self._wait(E, b.w)
        for b in writes:
            if b.w is not None:
                self._wait(E, b.w)
            for t in b.r:
                self._wait(E, t)

    def op(self, E, fn, reads=(), writes=()):
        self._deps(E, reads, writes)
        sem = E.cur_sem()
        ins = fn()
        E.count += 1
        ins.then_inc(sem, 1)
        t = (sem, E.count, E)
        E.seen[id(sem)] = max(E.seen.get(id(sem), 0), 0)
        for b in reads:
            b.r.append(t)
        for b in writes:
            b.w = t
            b.r = []
        return t

    def dma(self, Q, out, in_, reads=(), writes=(), is_output=False):
        self._deps(Q, reads, writes)
        i = self.dma_rr
        self.dma_rr = (self.dma_rr + 1) % len(self.dma_sems)
        sem = self.dma_sems[i]
        if self.dma_cnt[i] > 0:
            self._wait(Q, (sem, 16 * self.dma_cnt[i], None))
        ins = Q.eng.dma_start(out=out, in_=in_)
        self.dma_cnt[i] += 1
        ins.then_inc(sem, 16)
        t = (sem, 16 * self.dma_cnt[i], None)
        for b in reads:
            b.r.append(t)
        for b in writes:
            b.w = t
            b.r = []
        if is_output:
            self.out_tickets.append(t)
        return t

    def finish(self):
        for t in self.out_tickets:
            self._wait(self.sp, t)
        for i, sem in enumerate(self.dma_sems):
            if self.dma_cnt[i]:
                self._wait(self.sp, (sem, 16 * self.dma_cnt[i], None))


class Ring:
    def __init__(self, tiles):
        self.tiles = tiles
        self.bufs = [Buf() for _ in tiles]
        self.i = 0

    def next(self):
        k = self.i % len(self.tiles)
        self.i += 1
        return self.tiles[k], self.bufs[k]


_UID = [0]


def _uname(name):
    _UID[0] += 1
    return f"{name}_{_UID[0]}"


def sb(nc, stack, name, shape, dt):
    return stack.enter_context(nc.sbuf_tensor(_uname(name), shape, dt))


def ps(nc, stack, name, shape, dt=F32):
    return stack.enter_context(nc.psum_tensor(_uname(name), shape, dt))


def barrier(trk):
    engs = [trk.pe, trk.act, trk.dve, trk.pool, trk.sp]
    ticks = []
    for E in engs:
        if E.sem is not None and E.count > 0:
            ticks.append((E.sem, E.count, E))
    for i, sem in enumerate(trk.dma_sems):
        if trk.dma_cnt[i]:
            ticks.append((sem, 16 * trk.dma_cnt[i], None))
    for E in engs:
        for t in ticks:
            if t[2] is E:
                continue
            trk._wait(E, t)


def rstd_from_ss(nc, trk, ss_ps, ss_b, ms, ms_b, rstd, rstd_b, neghalf, nh_b, n, inv_d, eps):
    trk.op(trk.dve, lambda: nc.vector.tensor_scalar(out=ms[:, :n], in0=ss_ps[:, :n], scalar1=inv_d, scalar2=eps,
                                                    op0=ALU.mult, op1=ALU.add), reads=[ss_b], writes=[ms_b])
    trk.op(trk.pool, lambda: nc.gpsimd.tensor_tensor(out=rstd[:, :n], in0=ms[:, :n], in1=neghalf[:, :n], op=ALU.pow),
           reads=[ms_b, nh_b], writes=[rstd_b])


def phase_a(nc, trk, ExitStack, T, xT, w_in, gpre_d, ctab_d, stab_d, pm_d, qkT, v_out):
    KC = D_MODEL // 128
    NT = T // 512
    with ExitStack() as st:
        ones = sb(nc, st, "a_ones", [128, 128], BF16); ones_b = Buf()
        neghalf = sb(nc, st, "a_nh", [128, 512], F32); nh_b = Buf()
        gpre = sb(nc, st, "a_gpre", [128, KC], F32); gpre_b = Buf()
        pm = sb(nc, st, "a_pm", [128, 128], BF16); pm_b = Buf()
        ctab = sb(nc, st, "a_ctab", [128, T], F32); ctab_b = Buf()
        stab = sb(nc, st, "a_stab", [128, T], F32); stab_b = Buf()
        hT = sb(nc, st, "a_hT", [128, KC, T], BF16)
        hT_b = [Buf() for _ in range(NT)]
        trk.op(trk.dve, lambda: nc.vector.memset(ones[:], 1.0), writes=[ones_b])
        trk.op(trk.dve, lambda: nc.vector.memset(neghalf[:], -0.5), writes=[nh_b])
        trk.dma(trk.sp, gpre[:], gpre_d, writes=[gpre_b])
        trk.dma(trk.sp, pm[:], pm_d, writes=[pm_b])
        trk.dma(trk.sp, ctab[:], ctab_d, writes=[ctab_b])
        trk.dma(trk.sp, stab[:], stab_d, writes=[stab_b])
        xv = xT.rearrange("(kc p) t -> p kc t", p=128)
        with ExitStack() as s1:
            xr = Ring([sb(nc, s1, f"a_x{i}", [128, KC, 512], F32) for i in range(2)])
            sq = sb(nc, s1, "a_sq", [128, KC, 512], BF16); sq_b = Buf()
            ss = ps(nc, s1, "a_ss", [128, 512]); ss_b = Buf()
            ms = sb(nc, s1, "a_ms", [128, 512], F32); ms_b = Buf()
            rstd = sb(nc, s1, "a_rstd", [128, 512], F32); rstd_b = Buf()
            for tc in range(NT):
                xt, xb = xr.next()
                for q4 in range(4):
                    trk.dma(trk.sp, xt[:, q4 * 4:(q4 + 1) * 4, :], xv[:, q4 * 4:(q4 + 1) * 4, tc * 512:(tc + 1) * 512],
                            writes=[xb])
                trk.op(trk.act, lambda: nc.scalar.activation(out=sq[:], in_=xt[:], func=AF.Square),
                       reads=[xb], writes=[sq_b])
                for kc in range(KC):
                    trk.op(trk.pe, lambda: nc.tensor.matmul(ss[:], lhsT=ones[:], rhs=sq[:, kc, :],
                                                            start=(kc == 0), stop=(kc == KC - 1)),
                           reads=[ones_b, sq_b], writes=[ss_b])
                rstd_from_ss(nc, trk, ss, ss_b, ms, ms_b, rstd, rstd_b, neghalf, nh_b, 512, 1.0 / D_MODEL, NORM_EPS)
                for kc in range(KC):
                    trk.op(trk.dve, lambda: nc.vector.scalar_tensor_tensor(
                        out=hT[:, kc, tc * 512:(tc + 1) * 512], in0=xt[:, kc, :], scalar=gpre[:, kc:kc + 1],
                        in1=rstd[:], op0=ALU.mult, op1=ALU.mult),
                        reads=[xb, gpre_b, rstd_b], writes=[hT_b[tc]])
            barrier(trk)
        with ExitStack() as s2:
            wr = Ring([sb(nc, s2, f"a_w{i}", [128, KC, 512], BF16) for i in range(2)])
            mm = Ring([ps(nc, s2, f"a_mm{i}", [128, 512]) for i in range(3)])
            pq = Ring([ps(nc, s2, f"a_pq{i}", [128, 512]) for i in range(2)])
            qbr = Ring([sb(nc, s2, f"a_qb{i}", [128, 512], BF16) for i in range(3)])
            t1r = Ring([sb(nc, s2, f"a_t1{i}", [128, 512], F32) for i in range(2)])
            t2r = Ring([sb(nc, s2, f"a_t2{i}", [128, 512], F32) for i in range(2)])
            outr = Ring([sb(nc, s2, f"a_o{i}", [128, 512], BF16) for i in range(3)])
            wv = w_in.rearrange("(kc p) n -> p kc n", p=128)
            blocks = [(0, "rope", 0), (1, "rope", 512), (2, "rope", 1024), (3, "rope", 1536),
                      (6, "fm", 2048), (7, "fm", 2560), (8, "fm", 3072), (9, "fm", 3584),
                      (4, "v", 0), (5, "v", 512), (10, "v", 1024), (11, "v", 1536)]
            for blk, kind, base in blocks:
                wt, wb = wr.next()
                for q4 in range(4):
                    trk.dma(trk.pool, wt[:, q4 * 4:(q4 + 1) * 4, :], wv[:, q4 * 4:(q4 + 1) * 4, blk * 512:(blk + 1) * 512],
                            writes=[wb])
                if kind in ("rope", "fm"):
                    for s in range(4):
                        for tc in range(NT):
                            acc, acc_b = mm.next()
                            for kc in range(KC):
                                trk.op(trk.pe, lambda: nc.tensor.matmul(
                                    acc[:], lhsT=wt[:, kc, s * 128:(s + 1) * 128], rhs=hT[:, kc, tc * 512:(tc + 1) * 512],
                                    start=(kc == 0), stop=(kc == KC - 1)), reads=[wb, hT_b[tc]], writes=[acc_b])
                            row0 = base + s * 128
                            if kind == "fm":
                                ot, ob = outr.next()
                                trk.op(trk.act, lambda: nc.scalar.activation(out=ot[:], in_=acc[:], func=AF.Copy),
                                       reads=[acc_b], writes=[ob])
                            else:
                                qb, qbb = qbr.next()
                                trk.op(trk.act, lambda: nc.scalar.activation(out=qb[:], in_=acc[:], func=AF.Copy),
                                       reads=[acc_b], writes=[qbb])
                                pqt, pqb = pq.next()
                                trk.op(trk.pe, lambda: nc.tensor.matmul(pqt[:], lhsT=pm[:], rhs=qb[:], start=True, stop=True),
                                       reads=[pm_b, qbb], writes=[pqb])
                                t1, t1b = t1r.next()
                                t2, t2b = t2r.next()
                                trk.op(trk.dve, lambda: nc.vector.tensor_tensor(out=t1[:], in0=qb[:], in1=ctab[:, tc * 512:(tc + 1) * 512], op=ALU.mult),
                                       reads=[qbb, ctab_b], writes=[t1b])
                                trk.op(trk.dve, lambda: nc.vector.tensor_tensor(out=t2[:], in0=pqt[:], in1=stab[:, tc * 512:(tc + 1) * 512], op=ALU.mult),
                                       reads=[pqb, stab_b], writes=[t2b])
                                ot, ob = outr.next()
                                trk.op(trk.pool, lambda: nc.gpsimd.tensor_tensor(out=ot[:], in0=t1[:], in1=t2[:], op=ALU.add),
                                       reads=[t1b, t2b], writes=[ob])
                            trk.dma(trk.sp, qkT[row0:row0 + 128, tc * 512:(tc + 1) * 512], ot[:], reads=[ob], is_output=True)
                else:
                    for tt in range(T // 128):
                        acc, acc_b = mm.next()
                        for kc in range(KC):
                            trk.op(trk.pe, lambda: nc.tensor.matmul(
                                acc[:], lhsT=hT[:, kc, tt * 128:(tt + 1) * 128], rhs=wt[:, kc, :],
                                start=(kc == 0), stop=(kc == KC - 1)), reads=[wb, hT_b[tt // 4]], writes=[acc_b])
                        ot, ob = outr.next()
                        trk.op(trk.act, lambda: nc.scalar.activation(out=ot[:], in_=acc[:], func=AF.Copy),
                               reads=[acc_b], writes=[ob])
                        trk.dma(trk.sp, v_out[tt * 128:(tt + 1) * 128, base:base + 512], ot[:], reads=[ob], is_output=True)
            barrier(trk)


def feat_pk(v):
    v = np.asarray(v)
    return np.ascontiguousarray(v.reshape(-1, 128).T)


def rope_tables(pos):
    inv_freq = (np.float32(500000.0) ** (-np.arange(0, 16, 2, dtype=np.float32) / np.float32(16))).astype(np.float32)
    ang = pos.astype(np.float32)[None, :] * inv_freq[:, None]
    cos = np.cos(ang).astype(np.float32)
    sin = np.sin(ang).astype(np.float32)
    T = pos.shape[0]
    ctab = np.ones((128, T), np.float32)
    stab = np.zeros((128, T), np.float32)
    for p in range(128):
        i = p % 64
        if i < 8:
            ctab[p] = cos[i]
            stab[p] = -sin[i]
        elif i < 16:
            ctab[p] = cos[i - 8]
            stab[p] = sin[i - 8]
    return ctab, stab


def rope_perm():
    pm = np.zeros((128, 128), np.float32)
    for m in range(128):
        i = m % 64
        if i < 8:
            pm[m + 8, m] = 1.0
        elif i < 16:
            pm[m - 8, m] = 1.0
    return pm.astype(ml_dtypes.bfloat16)


def build_a(T=TOK):
    from contextlib import ExitStack
    nc = bass.Bass("TRN2", target_bir_lowering=False)
    xT = nc.dram_tensor("xT", [D_MODEL, T], F32, kind="ExternalInput").ap()
    w_in = nc.dram_tensor("w_in", [D_MODEL, D_IN], F32, kind="ExternalInput").ap()
    gpre = nc.dram_tensor("gpre", [128, 16], F32, kind="ExternalInput").ap()
    ctab = nc.dram_tensor("ctab", [128, T], F32, kind="ExternalInput").ap()
    stab = nc.dram_tensor("stab", [128, T], F32, kind="ExternalInput").ap()
    pm = nc.dram_tensor("pm", [128, 128], BF16, kind="ExternalInput").ap()
    qkT = nc.dram_tensor("qkT", [4096, T], BF16, kind="ExternalOutput").ap()
    v = nc.dram_tensor("v", [T, 2048], BF16, kind="ExternalOutput").ap()
    with ExitStack() as st:
        trk = Tracker(nc, st)
        phase_a(nc, trk, ExitStack, T, xT, w_in, gpre, ctab, stab, pm, qkT, v)
        trk.finish()
    return nc


DIFF_SCALE = 64 ** -0.5
SB_SCALE = 128 ** -0.5


def phase_b(nc, trk, ExitStack, S, dqT, dkT, dv, sqT, skT, sv, lam4_d, linit_d, gsub_d, ident_d, maskd_d, masks_d, oT_d, oT_s):
    NKB = S // 128
    NQC = S // 512
    pe, act, dve, pool, sp = trk.pe, trk.act, trk.dve, trk.pool, trk.sp
    with ExitStack() as st:
        ident = sb(nc, st, "b_ident", [128, 128], BF16); ident_b = Buf()
        maskd = sb(nc, st, "b_maskd", [128, 128], BF16); maskd_b = Buf()
        masks = sb(nc, st, "b_masks", [128, 128], BF16); masks_b = Buf()
        gsub = sb(nc, st, "b_gsub", [128, 128], F32); gsub_b = Buf()
        gsc = sb(nc, st, "b_gsc", [128, 128], F32); gsc_b = Buf()
        lam4 = sb(nc, st, "b_lam4", [128, 4, 64], F32); lam4_b = Buf()
        linit = sb(nc, st, "b_linit", [128, 1], F32); linit_b = Buf()
        cst = sb(nc, st, "b_cst", [128, 16], F32); cst_b = Buf()
        prod = sb(nc, st, "b_prod", [128, 2, 64], F32); prod_b = Buf()
        neghalf = sb(nc, st, "b_nh", [128, 1], F32); nh_b = Buf()
        trk.dma(sp, ident[:], ident_d, writes=[ident_b])
        trk.dma(sp, maskd[:], maskd_d, writes=[maskd_b])
        trk.dma(sp, masks[:], masks_d, writes=[masks_b])
        trk.dma(sp, gsub[:], gsub_d, writes=[gsub_b])
        trk.dma(sp, lam4[:], lam4_d, writes=[lam4_b])
        trk.dma(sp, linit[:], linit_d, writes=[linit_b])
        trk.op(dve, lambda: nc.vector.memset(neghalf[:], -0.5), writes=[nh_b])
        trk.op(dve, lambda: nc.vector.tensor_tensor(out=prod[:, 0, :], in0=lam4[:, 0, :], in1=lam4[:, 1, :], op=ALU.mult),
               reads=[lam4_b], writes=[prod_b])
        trk.op(dve, lambda: nc.vector.tensor_tensor(out=prod[:, 1, :], in0=lam4[:, 2, :], in1=lam4[:, 3, :], op=ALU.mult),
               reads=[lam4_b], writes=[prod_b])
        trk.op(dve, lambda: nc.vector.reduce_sum(out=cst[:, 2:3], in_=prod[:, 0, :], axis=mybir.AxisListType.X),
               reads=[prod_b], writes=[cst_b])
        trk.op(dve, lambda: nc.vector.reduce_sum(out=cst[:, 3:4], in_=prod[:, 1, :], axis=mybir.AxisListType.X),
               reads=[prod_b], writes=[cst_b])
        trk.op(act, lambda: nc.scalar.activation(out=cst[:, 4:6], in_=cst[:, 2:4], func=AF.Exp), reads=[cst_b], writes=[cst_b])
        trk.op(dve, lambda: nc.vector.tensor_tensor(out=cst[:, 6:7], in0=cst[:, 5:6], in1=cst[:, 4:5], op=ALU.subtract),
               reads=[cst_b], writes=[cst_b])
        trk.op(dve, lambda: nc.vector.tensor_tensor(out=cst[:, 0:1], in0=cst[:, 6:7], in1=linit[:, 0:1], op=ALU.subtract),
               reads=[cst_b, linit_b], writes=[cst_b])
        trk.op(dve, lambda: nc.vector.tensor_scalar(out=cst[:, 1:2], in0=linit[:, 0:1], scalar1=-1.0, scalar2=1.0,
                                                    op0=ALU.mult, op1=ALU.add), reads=[linit_b, cst_b], writes=[cst_b])
        trk.op(dve, lambda: nc.vector.tensor_scalar(out=gsc[:], in0=gsub[:], scalar1=cst[:, 1:2], scalar2=None, op0=ALU.mult),
               reads=[gsub_b, cst_b], writes=[gsc_b])
        neglam = cst[:, 0:1]

        qs = Ring([sb(nc, st, f"b_q{i}", [128, S], BF16) for i in range(2)])
        ks = Ring([sb(nc, st, f"b_k{i}", [128, S], BF16) for i in range(2)])
        vs = Ring([sb(nc, st, f"b_v{i}", [128, NKB, 129], BF16) for i in range(2)])
        for vt, vb in zip(vs.tiles, vs.bufs):
            trk.op(pool, lambda: nc.gpsimd.memset(vt[:, :, 128:129], 1.0), writes=[vb])
        stage_r = Ring([sb(nc, st, f"b_stage{i}", [128, 512], BF16) for i in range(2)])
        banks = [ps(nc, st, f"b_ps{i}", [128, 512]) for i in range(8)]

        def load_head(qd, kd, vd):
            qt, qb = qs.next(); kt, kb_ = ks.next(); vt, vb = vs.next()
            nsp = max(1, S // 2048)
            cw = S // nsp
            for a in range(nsp):
                trk.dma(sp, qt[:, a * cw:(a + 1) * cw], qd[:, a * cw:(a + 1) * cw], writes=[qb])
                trk.dma(sp, kt[:, a * cw:(a + 1) * cw], kd[:, a * cw:(a + 1) * cw], writes=[kb_])
            vdv = vd
            nv = max(1, NKB // 16)
            vw = NKB // nv
            for a in range(nv):
                trk.dma(sp, vt[:, a * vw:(a + 1) * vw, 0:128], vdv[:, a * vw:(a + 1) * vw, :], writes=[vb])
            return (qt, qb, kt, kb_, vt, vb)

        with ExitStack() as sd:
            pTr = Ring([sb(nc, sd, f"bd_pT{i}", [128, 512], BF16) for i in range(4)])
            rr_r = Ring([sb(nc, sd, f"bd_rr{i}", [128, 8], F32) for i in range(2)])
            t_r = Ring([sb(nc, sd, f"bd_t{i}", [128, 128], F32) for i in range(2)])
            o_r = Ring([sb(nc, sd, f"bd_o{i}", [128, 128], F32) for i in range(2)])
            junk_r = Ring([sb(nc, sd, f"bd_junk{i}", [128, 128], F32) for i in range(2)])
            on_r = Ring([sb(nc, sd, f"bd_on{i}", [128, 128], BF16) for i in range(2)])
            ps_s = [Ring([banks[0], banks[1]]), Ring([banks[2], banks[3]])]
            acc = [[None, None] for _ in range(4)]
            for j in range(4):
                for mp in range(2):
                    idx = j * 2 + mp
                    acc[j][mp] = (banks[4 + idx // 3][:, (idx % 3) * 129:(idx % 3) * 129 + 129], Buf())
            tp_r = Ring([banks[7][:, i * 128:(i + 1) * 128] for i in range(4)])
            for h in range(2):
                qt, qb, kt, kb_, vt, vb = load_head(dqT[h], dkT[h], dv[h])
                for c in range(NQC):
                    stage, stage_b = stage_r.next()
                    for bk in range(3):
                        accb = [acc[idx // 2][idx % 2][1] for idx in range(8) if idx // 3 == bk]
                        trk.op(dve, lambda: nc.vector.memset(banks[4 + bk][:, 0:387], 0.0), writes=accb)
                    for kb in range(4 * c + 4):
                        m = kb - 4 * c
                        j0 = max(m, 0)
                        N = 512 - 128 * j0
                        qoff = c * 512 + 128 * j0
                        for mp in range(2):
                            lo = 64 * mp
                            pst, psb = ps_s[mp].next()
                            kblk = kt[lo:lo + 64, kb * 128:(kb + 1) * 128]
                            if m >= 0:
                                trk.op(pe, lambda: nc.tensor.matmul(pst[:, 0:128], lhsT=ident[:], rhs=maskd[:], start=True, stop=False),
                                       reads=[ident_b, maskd_b], writes=[psb])
                                trk.op(pe, lambda: nc.tensor.matmul(pst[:, 0:128], lhsT=kblk, rhs=qt[lo:lo + 64, qoff:qoff + 128],
                                                                    start=False, stop=True), reads=[kb_, qb], writes=[psb])
                                if N > 128:
                                    trk.op(pe, lambda: nc.tensor.matmul(pst[:, 128:N], lhsT=kblk, rhs=qt[lo:lo + 64, qoff + 128:qoff + N],
                                                                        start=True, stop=True), reads=[kb_, qb], writes=[psb])
                            else:
                                trk.op(pe, lambda: nc.tensor.matmul(pst[:, 0:N], lhsT=kblk, rhs=qt[lo:lo + 64, qoff:qoff + N],
                                                                    start=True, stop=True), reads=[kb_, qb], writes=[psb])
                            pT, pTb = pTr.next()
                            trk.op(act, lambda: nc.scalar.activation(out=pT[:, :N], in_=pst[:, :N], func=AF.Exp, scale=DIFF_SCALE),
                                   reads=[psb], writes=[pTb])
                            for j in range(j0, 4):
                                a, ab = acc[j][mp]
                                trk.op(pe, lambda: nc.tensor.matmul(a, lhsT=pT[:, (j - j0) * 128:(j - j0 + 1) * 128], rhs=vt[:, kb, :],
                                                                    start=False, stop=(kb == 4 * c + j), skip_group_check=True),
                                       reads=[pTb, vb], writes=[ab])
                        if m >= 0:
                            j = m
                            a0, a0b = acc[j][0]
                            a1, a1b = acc[j][1]
                            rr, rrb = rr_r.next()
                            trk.op(dve, lambda: nc.vector.reciprocal(out=rr[:, 0:1], in_=a0[:, 128:129]), reads=[a0b], writes=[rrb])
                            trk.op(dve, lambda: nc.vector.reciprocal(out=rr[:, 1:2], in_=a1[:, 128:129]), reads=[a1b], writes=[rrb])
                            trk.op(dve, lambda: nc.vector.tensor_tensor(out=rr[:, 2:3], in0=rr[:, 1:2], in1=neglam, op=ALU.mult),
                                   reads=[rrb, cst_b], writes=[rrb])
                            trk.op(dve, lambda: nc.vector.memset(rr[:, 3:4], 0.0), writes=[rrb])
                            tt_, ttb = t_r.next()
                            trk.op(act, lambda: nc.scalar.activation(out=tt_[:], in_=a1[:, 0:128], func=AF.Copy, scale=rr[:, 2:3]),
                                   reads=[a1b, rrb], writes=[ttb])
                            o, ob = o_r.next()
                            trk.op(dve, lambda: nc.vector.scalar_tensor_tensor(out=o[:], in0=a0[:, 0:128], scalar=rr[:, 0:1], in1=tt_[:],
                                                                               op0=ALU.mult, op1=ALU.add),
                                   reads=[a0b, rrb, ttb], writes=[ob])
                            junk, junkb = junk_r.next()
                            trk.op(act, lambda: nc.scalar.activation(out=junk[:], in_=o[:], func=AF.Square, accum_out=rr[:, 3:4]),
                                   reads=[ob, rrb], writes=[junkb, rrb])
                            trk.op(dve, lambda: nc.vector.tensor_scalar(out=rr[:, 4:5], in0=rr[:, 3:4], scalar1=1.0 / 128, scalar2=SUBLN_EPS,
                                                                        op0=ALU.mult, op1=ALU.add), reads=[rrb], writes=[rrb])
                            trk.op(pool, lambda: nc.gpsimd.tensor_tensor(out=rr[:, 5:6], in0=rr[:, 4:5], in1=neghalf[:, 0:1], op=ALU.pow),
                                   reads=[rrb, nh_b], writes=[rrb])
                            on, onb = on_r.next()
                            trk.op(dve, lambda: nc.vector.scalar_tensor_tensor(out=on[:], in0=o[:], scalar=rr[:, 5:6], in1=gsc[:],
                                                                               op0=ALU.mult, op1=ALU.mult),
                                   reads=[ob, rrb, gsc_b], writes=[onb])
                            tp, tpb = tp_r.next()
                            trk.op(pe, lambda: nc.tensor.matmul(tp, lhsT=on[:], rhs=ident[:], start=True, stop=True),
                                   reads=[onb, ident_b], writes=[tpb])
                            trk.op(act, lambda: nc.scalar.activation(out=stage[:, j * 128:(j + 1) * 128], in_=tp, func=AF.Copy),
                                   reads=[tpb], writes=[stage_b])
                    trk.dma(sp, oT_d[h * 128:(h + 1) * 128, c * 512:(c + 1) * 512], stage[:], reads=[stage_b], is_output=True)
            barrier(trk)

        with ExitStack() as ss_:
            drows = [sb(nc, ss_, f"bs_dr{i}", [128, 1 + S], F32) for i in range(2)]
            NCH = S // 512
            drbufs = [[Buf() for _ in range(NCH + 1)] for _ in range(2)]
            for i in range(2):
                trk.op(pool, lambda: nc.gpsimd.memset(drows[i][:, 0:1], 1.0), writes=[drbufs[i][NCH]])
            sn_r = Ring([sb(nc, ss_, f"bs_sn{i}", [128, 512], F32) for i in range(2)])
            a_r = Ring([sb(nc, ss_, f"bs_a{i}", [128, 512], BF16) for i in range(2)])
            ats_r = Ring([sb(nc, ss_, f"bs_ats{i}", [128, 512], BF16) for i in range(2)])
            z_r = Ring([banks[0], banks[1], banks[2]])
            at_r = Ring([banks[3], banks[4], banks[5]])
            ot_r = Ring([banks[6][:, i * 128:(i + 1) * 128] for i in range(4)])
            for h in range(2):
                qt, qb, kt, kb_, vt, vb = load_head(sqT[h], skT[h], sv[h])
                for i in range(NKB):
                    L = 128 * (i + 1)
                    nch = (L + 511) // 512
                    di = i % 2
                    drow = drows[di]
                    db = drbufs[di]
                    trk._deps(dve, [], db[:NCH])
                    ot, otb = ot_r.next()
                    if i % 4 == 0:
                        stage, stage_b = stage_r.next()
                    first = True
                    qblk = qt[:, i * 128:(i + 1) * 128]
                    for cc in reversed(range(nch)):
                        k0 = 512 * cc
                        k1 = min(512 * (cc + 1), L)
                        w = k1 - k0
                        top = (cc == nch - 1)
                        z, zb = z_r.next()
                        if top:
                            trk.op(pe, lambda: nc.tensor.matmul(z[:, w - 128:w], lhsT=ident[:], rhs=masks[:], start=True, stop=False),
                                   reads=[ident_b, masks_b], writes=[zb])
                            trk.op(pe, lambda: nc.tensor.matmul(z[:, w - 128:w], lhsT=qblk, rhs=kt[:, k1 - 128:k1], start=False, stop=True),
                                   reads=[qb, kb_], writes=[zb])
                            if w > 128:
                                trk.op(pe, lambda: nc.tensor.matmul(z[:, 0:w - 128], lhsT=qblk, rhs=kt[:, k0:k1 - 128], start=True, stop=True),
                                       reads=[qb, kb_], writes=[zb])
                        else:
                            trk.op(pe, lambda: nc.tensor.matmul(z[:, 0:w], lhsT=qblk, rhs=kt[:, k0:k1], start=True, stop=True),
                                   reads=[qb, kb_], writes=[zb])
                        sn, snb = sn_r.next()
                        trk.op(act, lambda: nc.scalar.activation(out=sn[:, 0:w][:, ::-1], in_=z[:, 0:w], func=AF.Sigmoid, scale=-SB_SCALE),
                               reads=[zb], writes=[snb])
                        pstart = L - k1 + 1
                        prevb = db[NCH] if top else db[cc + 1]
                        trk.op(dve, lambda: nc.vector.tensor_tensor_scan(
                            out=drow[:, pstart:pstart + w], data0=sn[:, 0:w], data1=sn[:, 0:w],
                            initial=drow[:, pstart - 1:pstart], op0=ALU.mult, op1=ALU.bypass),
                            reads=[snb, prevb], writes=[db[cc]])
                        A, Ab = a_r.next()
                        trk.op(pool, lambda: nc.gpsimd.tensor_tensor(out=A[:, 0:w][:, ::-1], in0=drow[:, pstart - 1:pstart - 1 + w],
                                                                     in1=drow[:, pstart:pstart + w], op=ALU.subtract),
                               reads=[db[cc], prevb], writes=[Ab])
                        at, atb = at_r.next()
                        nb = w // 128
                        for b in range(nb):
                            trk.op(pe, lambda: nc.tensor.matmul(at[:, b * 128:(b + 1) * 128], lhsT=A[:, b * 128:(b + 1) * 128], rhs=ident[:],
                                                                start=True, stop=True), reads=[Ab, ident_b], writes=[atb])
                        ats, atsb = ats_r.next()
                        trk.op(act, lambda: nc.scalar.activation(out=ats[:, :w], in_=at[:, :w], func=AF.Copy), reads=[atb], writes=[atsb])
                        for b in range(nb):
                            kb = k0 // 128 + b
                            last = (cc == 0 and b == nb - 1)
                            trk.op(pe, lambda: nc.tensor.matmul(ot, lhsT=vt[:, kb, 0:128], rhs=ats[:, b * 128:(b + 1) * 128],
                                                                start=first, stop=last), reads=[vb, atsb], writes=[otb])
                            first = False
                    trk.op(act, lambda: nc.scalar.activation(out=stage[:, (i % 4) * 128:(i % 4 + 1) * 128], in_=ot, func=AF.Copy),
                           reads=[otb], writes=[stage_b])
                    if i % 4 == 3:
                        c = i // 4
                        trk.dma(sp, oT_s[h * 128:(h + 1) * 128, c * 512:(c + 1) * 512], stage[:], reads=[stage_b], is_output=True)
            barrier(trk)


def attn_masks():
    k = np.arange(128)[:, None]
    q = np.arange(128)[None, :]
    maskd = np.where(k <= q, 0.0, MASK_NEG).astype(np.float32)
    qq = np.arange(128)[:, None]
    kk = np.arange(128)[None, :]
    masks = np.where(kk >= qq, MASK_NEG, 0.0).astype(np.float32)
    ident = np.eye(128, dtype=np.float32)
    return (ident.astype(ml_dtypes.bfloat16), maskd.astype(ml_dtypes.bfloat16), masks.astype(ml_dtypes.bfloat16))


def build_b(S=SEQ):
    from contextlib import ExitStack
    nc = bass.Bass("TRN2", target_bir_lowering=False)
    NKB = S // 128
    d = {}
    for nm in ("dqT", "dkT", "sqT", "skT"):
        d[nm] = nc.dram_tensor(nm, [2, 128, S], BF16, kind="ExternalInput").ap()
    for nm in ("dv", "sv"):
        d[nm] = nc.dram_tensor(nm, [2, 128, NKB * 128], BF16, kind="ExternalInput").ap()
    lam4 = nc.dram_tensor("lam4", [128, 4, 64], F32, kind="ExternalInput").ap()
    linit = nc.dram_tensor("linit", [128, 1], F32, kind="ExternalInput").ap()
    gsub = nc.dram_tensor("gsub", [128, 128], F32, kind="ExternalInput").ap()
    ident = nc.dram_tensor("ident", [128, 128], BF16, kind="ExternalInput").ap()
    maskd = nc.dram_tensor("maskd", [128, 128], BF16, kind="ExternalInput").ap()
    masks = nc.dram_tensor("masks", [128, 128], BF16, kind="ExternalInput").ap()
    oT = nc.dram_tensor("oT", [512, S], BF16, kind="ExternalOutput").ap()
    with ExitStack() as st:
        trk = Tracker(nc, st)
        dvv = [d["dv"][h].rearrange("p (kb d) -> p kb d", d=128) for h in range(2)]
        svv = [d["sv"][h].rearrange("p (kb d) -> p kb d", d=128) for h in range(2)]
        phase_b(nc, trk, ExitStack, S, d["dqT"], d["dkT"], dvv, d["sqT"], d["skT"], svv, lam4, linit, gsub,
                ident, maskd, masks, oT[0:256, :], oT[256:512, :])
        trk.finish()
    return nc


def phase_c(nc, trk, ExitStack, T, oT, xT, w_out, w_up, w_down, gpa_d, gpf_d, gpo_d, cw_d, cb_d,
            xmid_s, h2_s, g_s, y_s, xoutT):
    KC = D_MODEL // 128
    TC = T + HALO
    NCH = 5
    CW = TC // NCH
    assert CW * NCH == TC and CW <= 512
    NFF = D_FF // 128
    pe, act, dve, pool, sp = trk.pe, trk.act, trk.dve, trk.pool, trk.sp
    with ExitStack() as st:
        ones = sb(nc, st, "c_ones", [128, 128], BF16); ones_b = Buf()
        neghalf = sb(nc, st, "c_nh", [128, 512], F32); nh_b = Buf()
        gpa = sb(nc, st, "c_gpa", [128, KC], F32); gpa_b = Buf()
        gpf = sb(nc, st, "c_gpf", [128, KC], F32); gpf_b = Buf()
        gpo = sb(nc, st, "c_gpo", [128, KC], F32); gpo_b = Buf()
        cw = sb(nc, st, "c_cw", [128, 2 * NFF, 3], F32); cw_b = Buf()
        cb = sb(nc, st, "c_cb", [128, 2 * NFF], F32); cb_b = Buf()
        trk.op(dve, lambda: nc.vector.memset(ones[:], 1.0), writes=[ones_b])
        trk.op(dve, lambda: nc.vector.memset(neghalf[:], -0.5), writes=[nh_b])
        trk.dma(sp, gpa[:], gpa_d, writes=[gpa_b])
        trk.dma(sp, gpf[:], gpf_d, writes=[gpf_b])
        trk.dma(sp, gpo[:], gpo_d, writes=[gpo_b])
        trk.dma(sp, cw[:], cw_d, writes=[cw_b])
        trk.dma(sp, cb[:], cb_d, writes=[cb_b])
        xmid_b = Buf(); h2s_b = Buf(); gs_b = Buf(); ys_b = Buf()
        ov = oT.rearrange("(kc p) t -> p kc t", p=128)
        xv = xT.rearrange("(kc p) t -> p kc t", p=128)
        xmv = xmid_s.rearrange("(kc p) t -> p kc t", p=128)
        h2v = h2_s.rearrange("(kc p) t -> p kc t", p=128)
        yv_ = y_s.rearrange("(kc p) t -> p kc t", p=128)
        xov = xoutT.rearrange("(kc p) t -> p kc t", p=128)

        with ExitStack() as s1:
            wo = sb(nc, s1, "c1_wo", [128, KC, D_MODEL], BF16); wo_b = Buf()
            wov = w_out.rearrange("(kc p) n -> p kc n", p=128)
            for nb in range(4):
                for q4 in range(4):
                    trk.dma(pool, wo[:, q4 * 4:(q4 + 1) * 4, nb * 512:(nb + 1) * 512],
                            wov[:, q4 * 4:(q4 + 1) * 4, nb * 512:(nb + 1) * 512], writes=[wo_b])
            ot_r = Ring([sb(nc, s1, f"c1_ot{i}", [128, KC, CW], BF16) for i in range(2)])
            xt_r = Ring([sb(nc, s1, f"c1_xt{i}", [128, KC, CW], F32) for i in range(1)])
            mix = sb(nc, s1, "c1_mix", [128, KC, CW], F32); mix_b = Buf()
            sq = sb(nc, s1, "c1_sq", [128, KC, CW], BF16); sq_b = Buf()
            h2c_r = Ring([sb(nc, s1, f"c1_h2c{i}", [128, KC, CW], BF16) for i in range(1)])
            ms = sb(nc, s1, "c1_ms", [128, 512], F32); ms_b = Buf()
            rstd = sb(nc, s1, "c1_rstd", [128, 512], F32); rstd_b = Buf()
            acc_r = Ring([ps(nc, s1, f"c1_acc{i}", [128, 512]) for i in range(3)])
            ss = ps(nc, s1, "c1_ss", [128, 512]); ss_b = Buf()
            ss2 = ps(nc, s1, "c1_ss2", [128, 512]); ss2_b = Buf()
            for ch in range(NCH):
                c0 = ch * CW
                ot, otb = ot_r.next()
                xt, xtb = xt_r.next()
                for q4 in range(4):
                    trk.dma(sp, ot[:, q4 * 4:(q4 + 1) * 4, :], ov[:, q4 * 4:(q4 + 1) * 4, c0:c0 + CW], writes=[otb])
                for q4 in range(4):
                    trk.dma(sp, xt[:, q4 * 4:(q4 + 1) * 4, :], xv[:, q4 * 4:(q4 + 1) * 4, c0:c0 + CW], writes=[xtb])
                for n in range(KC):
                    acc, accb = acc_r.next()
                    for kc in range(KC):
                        trk.op(pe, lambda: nc.tensor.matmul(acc[:, :CW], lhsT=wo[:, kc, n * 128:(n + 1) * 128], rhs=ot[:, kc, :],
                                                            start=(kc == 0), stop=(kc == KC - 1)), reads=[wo_b, otb], writes=[accb])
                    trk.op(act, lambda: nc.scalar.activation(out=mix[:, n, :], in_=acc[:, :CW], func=AF.Copy), reads=[accb], writes=[mix_b])
                    trk.op(act, lambda: nc.scalar.activation(out=sq[:, n, :], in_=acc[:, :CW], func=AF.Square), reads=[accb], writes=[sq_b])
                    trk.op(pe, lambda: nc.tensor.matmul(ss[:, :CW], lhsT=ones[:], rhs=sq[:, n, :], start=(n == 0), stop=(n == KC - 1)),
                           reads=[ones_b, sq_b], writes=[ss_b])
                rstd_from_ss(nc, trk, ss, ss_b, ms, ms_b, rstd, rstd_b, neghalf, nh_b, CW, 1.0 / D_MODEL, NORM_EPS)
                for n in range(KC):
                    trk.op(dve, lambda: nc.vector.tensor_tensor(out=mix[:, n, :], in0=mix[:, n, :], in1=rstd[:, :CW], op=ALU.mult),
                           reads=[rstd_b, mix_b], writes=[mix_b])
                    trk.op(dve, lambda: nc.vector.scalar_tensor_tensor(out=xt[:, n, :], in0=mix[:, n, :], scalar=gpa[:, n:n + 1], in1=xt[:, n, :],
                                                                       op0=ALU.mult, op1=ALU.add), reads=[mix_b, gpa_b, xtb], writes=[xtb])
                lo = max(c0, HALO)
                for q4 in range(4):
                    trk.dma(sp, xmv[:, q4 * 4:(q4 + 1) * 4, lo - HALO:c0 + CW - HALO], xt[:, q4 * 4:(q4 + 1) * 4, lo - c0:CW],
                            reads=[xtb], writes=[xmid_b])
                trk.op(act, lambda: nc.scalar.activation(out=sq[:], in_=xt[:], func=AF.Square), reads=[xtb], writes=[sq_b])
                for kc in range(KC):
                    trk.op(pe, lambda: nc.tensor.matmul(ss2[:, :CW], lhsT=ones[:], rhs=sq[:, kc, :], start=(kc == 0), stop=(kc == KC - 1)),
                           reads=[ones_b, sq_b], writes=[ss2_b])
                rstd_from_ss(nc, trk, ss2, ss2_b, ms, ms_b, rstd, rstd_b, neghalf, nh_b, CW, 1.0 / D_MODEL, NORM_EPS)
                h2c, h2cb = h2c_r.next()
                for kc in range(KC):
                    trk.op(dve, lambda: nc.vector.scalar_tensor_tensor(out=h2c[:, kc, :], in0=xt[:, kc, :], scalar=gpf[:, kc:kc + 1], in1=rstd[:, :CW],
                                                                       op0=ALU.mult, op1=ALU.mult), reads=[xtb, gpf_b, rstd_b], writes=[h2cb])
                for q4 in range(4):
                    trk.dma(sp, h2v[:, q4 * 4:(q4 + 1) * 4, c0:c0 + CW], h2c[:, q4 * 4:(q4 + 1) * 4, :], reads=[h2cb], writes=[h2s_b])
            barrier(trk)

        with ExitStack() as s2:
            h2T = sb(nc, s2, "c2_h2T", [128, KC, TC], BF16); h2T_b = Buf()
            for q4 in range(4):
                trk.dma(sp, h2T[:, q4 * 4:(q4 + 1) * 4, :], h2v[:, q4 * 4:(q4 + 1) * 4, :], reads=[h2s_b], writes=[h2T_b])
            wg_r = Ring([sb(nc, s2, f"c2_wg{i}", [128, KC, 256], BF16) for i in range(2)])
            wv_r = Ring([sb(nc, s2, f"c2_wv{i}", [128, KC, 256], BF16) for i in range(2)])
            ug_r = Ring([sb(nc, s2, f"c2_ug{i}", [128, TC], F32) for i in range(2)])
            uv_r = Ring([sb(nc, s2, f"c2_uv{i}", [128, TC], F32) for i in range(2)])
            yg_r = Ring([sb(nc, s2, f"c2_yg{i}", [128, T], F32) for i in range(1)])
            yv_r = Ring([sb(nc, s2, f"c2_yv{i}", [128, T], F32) for i in range(1)])
            gl_r = Ring([sb(nc, s2, f"c2_gl{i}", [128, T], F32) for i in range(1)])
            gt_r = Ring([sb(nc, s2, f"c2_gt{i}", [128, T], BF16) for i in range(2)])
            accg_r = Ring([ps(nc, s2, f"c2_ag{i}", [128, 512]) for i in range(3)])
            accv_r = Ring([ps(nc, s2, f"c2_av{i}", [128, 512]) for i in range(3)])
            wuv = w_up.rearrange("(kc p) n -> p kc n", p=128)
            NT8 = T // 256
            gsv = g_s.rearrange("(t p) (c w) -> t p c w", p=128, w=256)
            for grp in range(NFF // 2):
                wg, wgb = wg_r.next()
                wv, wvb = wv_r.next()
                for q4 in range(4):
                    trk.dma(pool, wg[:, q4 * 4:(q4 + 1) * 4, :], wuv[:, q4 * 4:(q4 + 1) * 4, grp * 256:(grp + 1) * 256], writes=[wgb])
                for q4 in range(4):
                    trk.dma(pool, wv[:, q4 * 4:(q4 + 1) * 4, :], wuv[:, q4 * 4:(q4 + 1) * 4, D_FF + grp * 256:D_FF + (grp + 1) * 256], writes=[wvb])
                for s in range(2):
                    cch = grp * 2 + s
                    ug, ugb = ug_r.next()
                    uv, uvb = uv_r.next()
                    for ch in range(NCH):
                        c0 = ch * CW
                        ag, agb = accg_r.next()
                        for kc in range(KC):
                            trk.op(pe, lambda: nc.tensor.matmul(ag[:, :CW], lhsT=wg[:, kc, s * 128:(s + 1) * 128], rhs=h2T[:, kc, c0:c0 + CW],
                                                                start=(kc == 0), stop=(kc == KC - 1)), reads=[wgb, h2T_b], writes=[agb])
                        trk.op(act, lambda: nc.scalar.activation(out=ug[:, c0:c0 + CW], in_=ag[:, :CW], func=AF.Copy), reads=[agb], writes=[ugb])
                        av, avb = accv_r.next()
                        for kc in range(KC):
                            trk.op(pe, lambda: nc.tensor.matmul(av[:, :CW], lhsT=wv[:, kc, s * 128:(s + 1) * 128], rhs=h2T[:, kc, c0:c0 + CW],
                                                                start=(kc == 0), stop=(kc == KC - 1)), reads=[wvb, h2T_b], writes=[avb])
                        trk.op(dve, lambda: nc.vector.tensor_copy(out=uv[:, c0:c0 + CW], in_=av[:, :CW]), reads=[avb], writes=[uvb])
                    yg, ygb = yg_r.next()
                    yv, yvb = yv_r.next()
                    cg = cch
                    cv = NFF + cch
                    trk.op(act, lambda: nc.scalar.activation(out=yg[:], in_=ug[:, 2:TC], func=AF.Identity, scale=cw[:, cg, 2:3], bias=cb[:, cg:cg + 1]),
                           reads=[ugb, cw_b, cb_b], writes=[ygb])
                    trk.op(dve, lambda: nc.vector.scalar_tensor_tensor(out=yg[:], in0=ug[:, 1:TC - 1], scalar=cw[:, cg, 1:2], in1=yg[:],
                                                                       op0=ALU.mult, op1=ALU.add), reads=[ugb, cw_b, ygb], writes=[ygb])
                    trk.op(dve, lambda: nc.vector.scalar_tensor_tensor(out=yg[:], in0=ug[:, 0:TC - 2], scalar=cw[:, cg, 0:1], in1=yg[:],
                                                                       op0=ALU.mult, op1=ALU.add), reads=[ugb, cw_b, ygb], writes=[ygb])
                    trk.op(pool, lambda: nc.gpsimd.tensor_scalar(out=yv[:], in0=uv[:, 2:TC], scalar1=cw[:, cv, 2:3], scalar2=cb[:, cv:cv + 1],
                                                                 op0=ALU.mult, op1=ALU.add), reads=[uvb, cw_b, cb_b], writes=[yvb])
                    trk.op(dve, lambda: nc.vector.scalar_tensor_tensor(out=yv[:], in0=uv[:, 1:TC - 1], scalar=cw[:, cv, 1:2], in1=yv[:],
                                                                       op0=ALU.mult, op1=ALU.add), reads=[uvb, cw_b, yvb], writes=[yvb])
                    trk.op(dve, lambda: nc.vector.scalar_tensor_tensor(out=yv[:], in0=uv[:, 0:TC - 2], scalar=cw[:, cv, 0:1], in1=yv[:],
                                                                       op0=ALU.mult, op1=ALU.add), reads=[uvb, cw_b, yvb], writes=[yvb])
                    gl, glb = gl_r.next()
                    trk.op(act, lambda: nc.scalar.activation(out=gl[:], in_=yg[:], func=AF.Gelu_apprx_tanh), reads=[ygb], writes=[glb])
                    gt, gtb = gt_r.next()
                    trk.op(pool, lambda: nc.gpsimd.tensor_tensor(out=gt[:], in0=gl[:], in1=yv[:], op=ALU.mult), reads=[glb, yvb], writes=[gtb])
                    for t8 in range(NT8):
                        trk.dma(sp, gsv[t8, :, cch, :], gt[:, t8 * 256:(t8 + 1) * 256], reads=[gtb], writes=[gs_b])
            barrier(trk)

        with ExitStack() as s3:
            ssb = [ps(nc, s3, f"c3_ss{i}", [128, 512]) for i in range(T // 512)]
            ss_bufs = [Buf() for _ in range(T // 512)]
            for i in range(T // 512):
                trk.op(dve, lambda: nc.vector.memset(ssb[i][:], 0.0), writes=[ss_bufs[i]])
            with ExitStack() as s3a:
                wd_r = Ring([sb(nc, s3a, f"c3_wd{i}", [128, NFF, 512], BF16) for i in range(2)])
                gt_r = Ring([sb(nc, s3a, f"c3_gt{i}", [128, NFF, 256], BF16) for i in range(2)])
                ysb_r = Ring([sb(nc, s3a, f"c3_y{i}", [128, 256], F32) for i in range(3)])
                sq_r = Ring([sb(nc, s3a, f"c3_sq{i}", [128, 256], BF16) for i in range(3)])
                acc_r = Ring([ps(nc, s3a, f"c3_acc{i}", [128, 512]) for i in range(3)])
                wdv = w_down.rearrange("(c p) n -> p c n", p=128)
                NT8 = T // 256
                gsv = g_s.rearrange("(t p) (c w) -> t p c w", p=128, w=256)
                for nb in range(4):
                    wd, wdb = wd_r.next()
                    for q in range(NFF // 4):
                        trk.dma(pool, wd[:, q * 4:(q + 1) * 4, :], wdv[:, q * 4:(q + 1) * 4, nb * 512:(nb + 1) * 512], writes=[wdb])
                    for t8 in range(NT8):
                        gt, gtb = gt_r.next()
                        for q in range(4):
                            trk.dma(sp, gt[:, q * 11:(q + 1) * 11, :], gsv[t8, :, q * 11:(q + 1) * 11, :], reads=[gs_b], writes=[gtb])
                        for nn in range(4):
                            n = nb * 4 + nn
                            acc, accb = acc_r.next()
                            for c in range(NFF):
                                trk.op(pe, lambda: nc.tensor.matmul(acc[:, :256], lhsT=wd[:, c, nn * 128:(nn + 1) * 128], rhs=gt[:, c, :],
                                                                    start=(c == 0), stop=(c == NFF - 1)), reads=[wdb, gtb], writes=[accb])
                            ysb, ysbb = ysb_r.next()
                            trk.op(act, lambda: nc.scalar.activation(out=ysb[:], in_=acc[:, :256], func=AF.Copy), reads=[accb], writes=[ysbb])
                            trk.dma(sp, y_s[n * 128:(n + 1) * 128, t8 * 256:(t8 + 1) * 256], ysb[:], reads=[ysbb], writes=[ys_b])
                            sq, sqb = sq_r.next()
                            trk.op(act, lambda: nc.scalar.activation(out=sq[:], in_=acc[:, :256], func=AF.Square), reads=[accb], writes=[sqb])
                            bi = (t8 * 256) // 512
                            off = (t8 * 256) % 512
                            trk.op(pe, lambda: nc.tensor.matmul(ssb[bi][:, off:off + 256], lhsT=ones[:], rhs=sq[:], start=False, stop=(n == KC - 1),
                                                                skip_group_check=True), reads=[ones_b, sqb], writes=[ss_bufs[bi]])
                barrier(trk)
            with ExitStack() as s3b:
                ms = sb(nc, s3b, "c3_ms", [128, 512], F32); ms_b = Buf()
                rstd_r = Ring([sb(nc, s3b, f"c3_rstd{i}", [128, 512], F32) for i in range(2)])
                y_r = Ring([sb(nc, s3b, f"c3_yy{i}", [128, KC, 512], F32) for i in range(2)])
                xm_r = Ring([sb(nc, s3b, f"c3_xm{i}", [128, KC, 512], F32) for i in range(2)])
                for t4 in range(T // 512):
                    rstd, rstd_b = rstd_r.next()
                    rstd_from_ss(nc, trk, ssb[t4], ss_bufs[t4], ms, ms_b, rstd, rstd_b, neghalf, nh_b, 512, 1.0 / D_MODEL, NORM_EPS)
                    yy, yyb = y_r.next()
                    xm, xmb = xm_r.next()
                    for q4 in range(4):
                        trk.dma(sp, yy[:, q4 * 4:(q4 + 1) * 4, :], yv_[:, q4 * 4:(q4 + 1) * 4, t4 * 512:(t4 + 1) * 512], reads=[ys_b], writes=[yyb])
                        trk.dma(sp, xm[:, q4 * 4:(q4 + 1) * 4, :], xmv[:, q4 * 4:(q4 + 1) * 4, t4 * 512:(t4 + 1) * 512], reads=[xmid_b], writes=[xmb])
                    for n in range(KC):
                        trk.op(dve, lambda: nc.vector.tensor_tensor(out=yy[:, n, :], in0=yy[:, n, :], in1=rstd[:], op=ALU.mult),
                               reads=[rstd_b, yyb], writes=[yyb])
                        trk.op(dve, lambda: nc.vector.scalar_tensor_tensor(out=xm[:, n, :], in0=yy[:, n, :], scalar=gpo[:, n:n + 1], in1=xm[:, n, :],
                                                                           op0=ALU.mult, op1=ALU.add), reads=[yyb, gpo_b, xmb], writes=[xmb])
                    for q4 in range(4):
                        trk.dma(sp, xov[:, q4 * 4:(q4 + 1) * 4, t4 * 512:(t4 + 1) * 512], xm[:, q4 * 4:(q4 + 1) * 4, :], reads=[xmb], is_output=True)
                barrier(trk)


def build_c(T=TOK):
    from contextlib import ExitStack
    nc = bass.Bass("TRN2", target_bir_lowering=False)
    TC = T + HALO
    oT = nc.dram_tensor("oT", [D_MODEL, TC], BF16, kind="ExternalInput").ap()
    xT = nc.dram_tensor("xT", [D_MODEL, TC], F32, kind="ExternalInput").ap()
    w_out = nc.dram_tensor("w_out", [D_MODEL, D_MODEL], F32, kind="ExternalInput").ap()
    w_up = nc.dram_tensor("w_up", [D_MODEL, 2 * D_FF], F32, kind="ExternalInput").ap()
    w_down = nc.dram_tensor("w_down", [D_FF, D_MODEL], F32, kind="ExternalInput").ap()
    gpa = nc.dram_tensor("gpa", [128, 16], F32, kind="ExternalInput").ap()
    gpf = nc.dram_tensor("gpf", [128, 16], F32, kind="ExternalInput").ap()
    gpo = nc.dram_tensor("gpo", [128, 16], F32, kind="ExternalInput").ap()
    cw = nc.dram_tensor("cw", [128, 88, 3], F32, kind="ExternalInput").ap()
    cb = nc.dram_tensor("cb", [128, 88], F32, kind="ExternalInput").ap()
    xmid_s = nc.dram_tensor("xmid_s", [D_MODEL, T], F32).ap()
    h2_s = nc.dram_tensor("h2_s", [D_MODEL, TC], BF16).ap()
    g_s = nc.dram_tensor("g_s", [(T // 256) * 128, (D_FF // 128) * 256], BF16).ap()
    y_s = nc.dram_tensor("y_s", [D_MODEL, T], F32).ap()
    xoutT = nc.dram_tensor("xoutT", [D_MODEL, T], F32, kind="ExternalOutput").ap()
    with ExitStack() as st:
        trk = Tracker(nc, st)
        phase_c(nc, trk, ExitStack, T, oT, xT, w_out, w_up, w_down, gpa, gpf, gpo, cw, cb, xmid_s, h2_s, g_s, y_s, xoutT)
        trk.finish()
    return nc


_NC_CACHE = {}


def _get_nc(name):
    if name not in _NC_CACHE:
        _NC_CACHE[name] = {"a": build_a, "b": build_b, "c": build_c}[name]()
    return _NC_CACHE[name]


def _vlay(v):
    S = v.shape[0]
    return np.ascontiguousarray(v.reshape(S // 128, 128, 128).transpose(1, 0, 2).reshape(128, S))


def kernel_unfused(x, attn_pre_norm, w_in, diff_lambda_q1, diff_lambda_k1, diff_lambda_q2, diff_lambda_k2, diff_subln,
           w_out, attn_post_norm, ffn_pre_norm, ffn_w_up, ffn_conv_w, ffn_conv_b, ffn_w_down, ffn_post_norm, _depth=DEPTH):
    f32 = np.float32
    x = np.asarray(x, f32)
    G = 4
    cores = list(range(NCORES))
    xT = [np.ascontiguousarray(x[c // G, (c % G) * TOK:(c % G + 1) * TOK].T) for c in cores]
    pm = rope_perm()
    tabs = [rope_tables(np.arange((c % G) * TOK, (c % G + 1) * TOK)) for c in cores]
    ident, maskd, masks = attn_masks()
    ncA, ncB, ncC = _get_nc("a"), _get_nc("b"), _get_nc("c")
    for l in range(_depth):
        w_in_l = np.ascontiguousarray(np.asarray(w_in[l], f32))
        gpre = feat_pk(np.asarray(attn_pre_norm[l], f32))
        in_a = [{"xT": xT[c], "w_in": w_in_l, "gpre": gpre, "ctab": tabs[c][0], "stab": tabs[c][1], "pm": pm} for c in cores]
        ra = run_bass_kernel_spmd(ncA, in_a, core_ids=cores).results
        lam4 = np.ascontiguousarray(np.broadcast_to(np.stack([np.asarray(a[l], f32) for a in
                                    (diff_lambda_q1, diff_lambda_k1, diff_lambda_q2, diff_lambda_k2)])[None], (128, 4, 64)))
        lin = 0.8 - 0.6 * float(np.exp(-0.3 * l))
        linit = np.full((128, 1), lin, f32)
        gsub = np.ascontiguousarray(np.broadcast_to(np.asarray(diff_subln[l], f32)[None], (128, 128)))
        in_b = []
        for c in cores:
            b, g = c // G, c % G
            src = [ra[b * G + gg] for gg in range(G)]
            d = {"lam4": lam4, "linit": linit, "gsub": gsub, "ident": ident, "maskd": maskd, "masks": masks}
            for nm, base in (("dqT", 0), ("dkT", 1024), ("sqT", 2048), ("skT", 3072)):
                d[nm] = np.ascontiguousarray(np.stack([
                    np.concatenate([s["qkT"][base + (2 * g + hh) * 128: base + (2 * g + hh + 1) * 128] for s in src], axis=1)
                    for hh in range(2)]))
            for nm, base in (("dv", 0), ("sv", 1024)):
                d[nm] = np.ascontiguousarray(np.stack([
                    _vlay(np.concatenate([s["v"][:, base + (2 * g + hh) * 128: base + (2 * g + hh + 1) * 128] for s in src], axis=0))
                    for hh in range(2)]))
            in_b.append(d)
        rb = run_bass_kernel_spmd(ncB, in_b, core_ids=cores).results
        oT_full = []
        for b in range(BATCH):
            o = np.zeros((D_MODEL, HALO + SEQ), rb[0]["oT"].dtype)
            for g in range(G):
                r = rb[b * G + g]["oT"]
                o[2 * g * 128:(2 * g + 2) * 128, HALO:] = r[0:256]
                o[1024 + 2 * g * 128:1024 + (2 * g + 2) * 128, HALO:] = r[256:512]
            oT_full.append(o)
        cwl = np.asarray(ffn_conv_w[l], f32)
        common = {"w_out": np.ascontiguousarray(np.asarray(w_out[l], f32)),
                  "w_up": np.ascontiguousarray(np.asarray(ffn_w_up[l], f32)),
                  "w_down": np.ascontiguousarray(np.asarray(ffn_w_down[l], f32)),
                  "gpa": feat_pk(np.asarray(attn_post_norm[l], f32)), "gpf": feat_pk(np.asarray(ffn_pre_norm[l], f32)),
                  "gpo": feat_pk(np.asarray(ffn_post_norm[l], f32)),
                  "cw": np.ascontiguousarray(np.stack([feat_pk(cwl[k]) for k in range(3)], axis=-1)),
                  "cb": feat_pk(np.asarray(ffn_conv_b[l], f32))}
        in_c = []
        for c in cores:
            b, g = c // G, c % G
            xh = np.zeros((D_MODEL, HALO + TOK), f32)
            xh[:, HALO:] = xT[c]
            if g > 0:
                xh[:, :HALO] = xT[c - 1][:, TOK - HALO:]
            d = dict(common)
            d["xT"] = xh
            d["oT"] = np.ascontiguousarray(oT_full[b][:, g * TOK:g * TOK + TOK + HALO])
            in_c.append(d)
        rc = run_bass_kernel_spmd(ncC, in_c, core_ids=cores).results
        xT = [np.ascontiguousarray(rc[c]["xoutT"]) for c in cores]
    out = np.empty((BATCH, SEQ, D_MODEL), f32)
    for c in cores:
        out[c // G, (c % G) * TOK:(c % G + 1) * TOK] = xT[c].T
    return out


def build_fused(depth=DEPTH, S=SEQ):
    from contextlib import ExitStack
    nc = bass.Bass("TRN2", target_bir_lowering=False)
    T = TOK
    NQ = S // T
    ei = lambda nm, shp, dt=F32: nc.dram_tensor(nm, shp, dt, kind="ExternalInput").ap()
    x0 = ei("x0", [D_MODEL, HALO + S])
    w_in = ei("w_in", [depth, D_MODEL, D_IN])
    w_out = ei("w_out", [depth, D_MODEL, D_MODEL])
    w_up = ei("w_up", [depth, D_MODEL, 2 * D_FF])
    w_down = ei("w_down", [depth, D_FF, D_MODEL])
    gpre = ei("gpre", [depth, 128, 16]); gpa = ei("gpa", [depth, 128, 16]); gpf = ei("gpf", [depth, 128, 16]); gpo = ei("gpo", [depth, 128, 16])
    cw = ei("cw", [depth, 128, 88, 3]); cb = ei("cb", [depth, 128, 88])
    lam4 = ei("lam4", [depth, 128, 4, 64]); linit = ei("linit", [depth, 128, 1]); gsub = ei("gsub", [depth, 128, 128])
    ctab = ei("ctab", [128, S]); stab = ei("stab", [128, S])
    pm = ei("pm", [128, 128], BF16); ident = ei("ident", [128, 128], BF16); maskd = ei("maskd", [128, 128], BF16); masks = ei("masks", [128, 128], BF16)
    xoutT = nc.dram_tensor("xoutT", [D_MODEL, S], F32, kind="ExternalOutput").ap()
    xbuf = [nc.dram_tensor(f"xbuf{i}", [D_MODEL, HALO + S], F32).ap() for i in range(2)]
    qkT_full = nc.dram_tensor("qkT_full", [4096, S], BF16).ap()
    v_full = nc.dram_tensor("v_full", [S, 2048], BF16).ap()
    o_full = nc.dram_tensor("o_full", [D_MODEL, HALO + S], BF16).ap()
    xmid_s = nc.dram_tensor("xmid_s", [D_MODEL, T], F32).ap()
    h2_s = nc.dram_tensor("h2_s", [D_MODEL, T + HALO], BF16).ap()
    g_s = nc.dram_tensor("g_s", [(T // 256) * 128, (D_FF // 128) * 256], BF16).ap()
    y_s = nc.dram_tensor("y_s", [D_MODEL, T], F32).ap()
    with ExitStack() as st:
        trk = Tracker(nc, st)
        with ExitStack() as sz:
            zf = sb(nc, sz, "z_f", [128, 16, HALO], F32); zfb = Buf()
            zb = sb(nc, sz, "z_b", [128, 16, HALO], BF16); zbb = Buf()
            trk.op(trk.dve, lambda: nc.vector.memset(zf[:], 0.0), writes=[zfb])
            trk.op(trk.dve, lambda: nc.vector.memset(zb[:], 0.0), writes=[zbb])
            for i in range(2):
                trk.dma(trk.sp, xbuf[i].rearrange("(kc p) t -> p kc t", p=128)[:, :, 0:HALO], zf[:], reads=[zfb])
            trk.dma(trk.sp, o_full.rearrange("(kc p) t -> p kc t", p=128)[:, :, 0:HALO], zb[:], reads=[zbb])
            barrier(trk)
        for l in range(depth):
            xin = x0 if l == 0 else xbuf[(l - 1) % 2]
            for q in range(NQ):
                phase_a(nc, trk, ExitStack, T, xin[:, HALO + q * T:HALO + (q + 1) * T], w_in[l], gpre[l],
                        ctab[:, q * T:(q + 1) * T], stab[:, q * T:(q + 1) * T], pm,
                        qkT_full[:, q * T:(q + 1) * T], v_full[q * T:(q + 1) * T, :])
            for g in range(4):
                hp = lambda base: qkT_full[base + g * 256:base + (g + 1) * 256, :].rearrange("(h p) s -> h p s", p=128)
                vv = lambda base: [v_full[:, base + (2 * g + hh) * 128:base + (2 * g + hh + 1) * 128].rearrange("(kb p) d -> p kb d", p=128)
                                   for hh in range(2)]
                phase_b(nc, trk, ExitStack, S, hp(0), hp(1024), vv(0), hp(2048), hp(3072), vv(1024), lam4[l], linit[l], gsub[l],
                        ident, maskd, masks, o_full[g * 256:(g + 1) * 256, HALO:], o_full[1024 + g * 256:1024 + (g + 1) * 256, HALO:])
            for q in range(NQ):
                xo = xoutT[:, q * T:(q + 1) * T] if l == depth - 1 else xbuf[l % 2][:, HALO + q * T:HALO + (q + 1) * T]
                phase_c(nc, trk, ExitStack, T, o_full[:, q * T:q * T + T + HALO], xin[:, q * T:q * T + T + HALO],
                        w_out[l], w_up[l], w_down[l], gpa[l], gpf[l], gpo[l], cw[l], cb[l], xmid_s, h2_s, g_s, y_s, xo)
        trk.finish()
    return nc


def fused_inputs(x_b, attn_pre_norm, w_in, diff_lambda_q1, diff_lambda_k1, diff_lambda_q2, diff_lambda_k2, diff_subln,
                 w_out, attn_post_norm, ffn_pre_norm, ffn_w_up, ffn_conv_w, ffn_conv_b, ffn_w_down, ffn_post_norm, depth):
    f32 = np.float32
    S = x_b.shape[0]
    x0 = np.zeros((D_MODEL, HALO + S), f32)
    x0[:, HALO:] = np.asarray(x_b, f32).T
    ctab, stab = rope_tables(np.arange(S))
    ident, maskd, masks = attn_masks()
    A = lambda a: np.ascontiguousarray(np.asarray(a[:depth], f32))
    pk = lambda a: np.ascontiguousarray(np.stack([feat_pk(np.asarray(a[l], f32)) for l in range(depth)]))
    cwv = np.ascontiguousarray(np.stack([np.stack([feat_pk(np.asarray(ffn_conv_w[l][k], f32)) for k in range(3)], axis=-1)
                                         for l in range(depth)]))
    lam4 = np.ascontiguousarray(np.stack([np.broadcast_to(np.stack([np.asarray(a[l], f32) for a in
                                (diff_lambda_q1, diff_lambda_k1, diff_lambda_q2, diff_lambda_k2)])[None], (128, 4, 64)) for l in range(depth)]))
    linit = np.stack([np.full((128, 1), 0.8 - 0.6 * float(np.exp(-0.3 * l)), f32) for l in range(depth)])
    gsub = np.ascontiguousarray(np.stack([np.broadcast_to(np.asarray(diff_subln[l], f32)[None], (128, 128)) for l in range(depth)]))
    return {"x0": x0, "w_in": A(w_in), "w_out": A(w_out), "w_up": A(ffn_w_up), "w_down": A(ffn_w_down),
            "gpre": pk(attn_pre_norm), "gpa": pk(attn_post_norm), "gpf": pk(ffn_pre_norm), "gpo": pk(ffn_post_norm),
            "cw": cwv, "cb": pk(ffn_conv_b), "lam4": lam4, "linit": linit, "gsub": gsub,
            "ctab": ctab, "stab": stab, "pm": rope_perm(), "ident": ident, "maskd": maskd, "masks": masks}


def kernel_fused(x, attn_pre_norm, w_in, diff_lambda_q1, diff_lambda_k1, diff_lambda_q2, diff_lambda_k2, diff_subln,
                 w_out, attn_post_norm, ffn_pre_norm, ffn_w_up, ffn_conv_w, ffn_conv_b, ffn_w_down, ffn_post_norm, _depth=DEPTH):
    x = np.asarray(x, np.float32)
    key = ("fused", _depth)
    if key not in _NC_CACHE:
        _NC_CACHE[key] = build_fused(_depth)
    nc = _NC_CACHE[key]
    in_maps = []
    for b in range(BATCH):
        in_maps.append(fused_inputs(x[b], attn_pre_norm, w_in, diff_lambda_q1, diff_lambda_k1, diff_lambda_q2, diff_lambda_k2,
                                    diff_subln, w_out, attn_post_norm, ffn_pre_norm, ffn_w_up, ffn_conv_w, ffn_conv_b,
                                    ffn_w_down, ffn_post_norm, _depth))
    res = run_bass_kernel_spmd(nc, in_maps, core_ids=list(range(BATCH))).results
    out = np.empty((BATCH, SEQ, D_MODEL), np.float32)
    for b in range(BATCH):
        out[b] = res[b]["xoutT"].T
    return out


kernel = kernel_fused
```
